# Optimizing a Trainium2 kernel written in Bass

```python
import math
import jax, jax.numpy as jnp
from jax import lax
import numpy as np

D_MODEL = 1024
BATCH = 2
SEQ = 16384
DEPTH = 4

N_MIXERS = 2
D_FF = 2816
HG_HEADS = 8
HG_KEY = D_MODEL // HG_HEADS
HG_VAL = D_MODEL // HG_HEADS
HG_CHUNK = 64
FOX_HEADS = 16
FOX_HEAD_DIM = D_MODEL // FOX_HEADS
Q_BLOCK = 128
N_HG_LAYERS = (DEPTH + 1) // 2
N_FOX_LAYERS = DEPTH // 2
EPS = 1e-6
MASK_VALUE = -1e30
MIN_GATE = 1e-30

kernel_name = "hgrn2_fox_macaron_interleaved"


def rmsnorm(x, g):
    x32 = x.astype(jnp.float32)
    y = x32 * lax.rsqrt(jnp.mean(x32 * x32, axis=-1, keepdims=True) + EPS)
    return (y * g.astype(jnp.float32)).astype(x.dtype)


def swiglu(h, w_in, w_out):
    gate, up = jnp.split(h @ w_in, 2, axis=-1)
    return (jax.nn.silu(gate) * up) @ w_out


def hgrn2_mixer(h, w_in, lb, out_norm_g, w_out):
    B, T, _ = h.shape
    C = HG_CHUNK
    n = T // C
    q, z_f, i, g = jnp.split(h @ w_in, 4, axis=-1)
    z32 = z_f.astype(jnp.float32)
    lb = lb.astype(jnp.float32)
    f = lb + (1.0 - lb) * jax.nn.sigmoid(z32)
    log_f = jnp.log(jnp.maximum(f, MIN_GATE))
    k = (1.0 - lb) * jax.nn.sigmoid(-z32)
    q = jax.nn.silu(q.astype(jnp.float32))
    v = i.astype(jnp.float32)

    def to_chunks(a, d):
        return a.reshape(B, n, C, HG_HEADS, d).transpose(1, 0, 3, 2, 4)

    qc, kc, lfc = to_chunks(q, HG_KEY), to_chunks(k, HG_KEY), to_chunks(log_f, HG_KEY)
    vc = to_chunks(v, HG_VAL)
    causal = jnp.tril(jnp.ones((C, C), dtype=bool))[:, :, None]

    def step(S, inp):
        qb, kb, vb, lb_c = inp
        b = jnp.cumsum(lb_c, axis=2)
        o_inter = jnp.einsum('bhtk,bhkv->bhtv', qb * jnp.exp(b), S)
        diff = b[:, :, :, None, :] - b[:, :, None, :, :]
        decay = jnp.where(causal, jnp.exp(jnp.where(causal, diff, 0.0)), 0.0)
        scores = jnp.einsum('bhtk,bhtsk,bhsk->bhts', qb, decay, kb)
        o_intra = jnp.einsum('bhts,bhsv->bhtv', scores, vb)
        b_last = b[:, :, -1:, :]
        k_dec = kb * jnp.exp(b_last - b)
        S_new = jnp.exp(b_last[:, :, 0, :])[..., None] * S + jnp.einsum('bhsk,bhsv->bhkv', k_dec, vb)
        return S_new, o_inter + o_intra

    S0 = jnp.zeros((B, HG_HEADS, HG_KEY, HG_VAL), jnp.float32)
    _, o = lax.scan(step, S0, (qc, kc, vc, lfc))
    o = o.transpose(1, 0, 3, 2, 4).reshape(B, T, HG_HEADS, HG_VAL)
    o = rmsnorm(o, out_norm_g).reshape(B, T, D_MODEL)
    o = o * jax.nn.silu(g.astype(jnp.float32))
    return o.astype(h.dtype) @ w_out


def fox_mixer(h, w_in, b_f, q_norm_g, k_norm_g, w_out):
    B, T, _ = h.shape
    H, Dh = FOX_HEADS, FOX_HEAD_DIM
    nb = T // Q_BLOCK
    D = D_MODEL
    proj = h @ w_in
    q, k, v, g, z_f = jnp.split(proj, [D, 2 * D, 3 * D, 4 * D], axis=-1)
    q = rmsnorm(q.reshape(B, T, H, Dh), q_norm_g).transpose(0, 2, 1, 3)
    k = rmsnorm(k.reshape(B, T, H, Dh), k_norm_g).transpose(0, 2, 1, 3)
    v = v.reshape(B, T, H, Dh).transpose(0, 2, 1, 3)
    log_f = jax.nn.log_sigmoid(z_f.astype(jnp.float32) + b_f.astype(jnp.float32))
    c = jnp.cumsum(log_f, axis=1).transpose(0, 2, 1)
    qb = q.reshape(B, H, nb, Q_BLOCK, Dh).transpose(2, 0, 1, 3, 4)
    cq = c.reshape(B, H, nb, Q_BLOCK).transpose(2, 0, 1, 3)
    scale = 1.0 / math.sqrt(Dh)
    key_pos = jnp.arange(T)

    def block(args):
        qi, cqi, start = args
        s = jnp.einsum('bhqd,bhkd->bhqk', qi, k).astype(jnp.float32) * scale
        s = s + cqi[..., None] - c[:, :, None, :]
        qpos = start + jnp.arange(Q_BLOCK)
        s = jnp.where(key_pos[None, :] <= qpos[:, None], s, MASK_VALUE)
        p = jax.nn.softmax(s, axis=-1)
        return jnp.einsum('bhqk,bhkd->bhqd', p.astype(v.dtype), v)

    o = lax.map(block, (qb, cq, jnp.arange(nb) * Q_BLOCK))
    o = o.transpose(1, 0, 3, 2, 4).reshape(B, T, D)
    o = o * jax.nn.sigmoid(g)
    return o @ w_out


def setup_inputs(seed: int = 0) -> dict:
    key = jax.random.key(seed)
    ks = jax.random.split(key, 16)
    f32 = jnp.float32
    D = D_MODEL

    def w(k, shape, fan_in):
        return jax.random.normal(k, shape, f32) * (fan_in ** -0.5)

    return {
        "x": jax.random.normal(ks[0], (BATCH, SEQ, D), f32),
        "norm_g": 1.0 + 0.05 * jax.random.normal(ks[1], (DEPTH, 3, D), f32),
        "ffn_w_in": w(ks[2], (DEPTH, 2, D, 2 * D_FF), D),
        "ffn_w_out": w(ks[3], (DEPTH, 2, D_FF, D), D_FF),
        "hg_w_in": w(ks[4], (N_HG_LAYERS, D, 4 * D), D),
        "hg_lb_logits": 0.5 * jax.random.normal(ks[5], (N_HG_LAYERS, HG_HEADS * HG_KEY), f32),
        "hg_out_norm_g": 1.0 + 0.05 * jax.random.normal(ks[6], (N_HG_LAYERS, HG_VAL), f32),
        "hg_w_out": w(ks[7], (N_HG_LAYERS, D, D), D),
        "fox_w_in": w(ks[8], (N_FOX_LAYERS, D, 4 * D + FOX_HEADS), D),
        "fox_b_f": 2.0 + 0.5 * jax.random.normal(ks[9], (N_FOX_LAYERS, FOX_HEADS), f32),
        "fox_q_norm_g": 1.0 + 0.05 * jax.random.normal(ks[10], (N_FOX_LAYERS, FOX_HEAD_DIM), f32),
        "fox_k_norm_g": 1.0 + 0.05 * jax.random.normal(ks[11], (N_FOX_LAYERS, FOX_HEAD_DIM), f32),
        "fox_w_out": w(ks[12], (N_FOX_LAYERS, D, D), D),
    }


def reference(x, norm_g, ffn_w_in, ffn_w_out, hg_w_in, hg_lb_logits, hg_out_norm_g, hg_w_out,
              fox_w_in, fox_b_f, fox_q_norm_g, fox_k_norm_g, fox_w_out):
    p = jax.nn.softmax(hg_lb_logits.astype(jnp.float32), axis=0)
    lbs = jnp.cumsum(p, axis=0) - p[0]
    for layer in range(DEPTH):
        j = layer // N_MIXERS
        x = x + 0.5 * swiglu(rmsnorm(x, norm_g[layer, 0]), ffn_w_in[layer, 0], ffn_w_out[layer, 0])
        hn = rmsnorm(x, norm_g[layer, 1])
        if layer % N_MIXERS == 0:
            x = x + hgrn2_mixer(hn, hg_w_in[j], lbs[j], hg_out_norm_g[j], hg_w_out[j])
        else:
            x = x + fox_mixer(hn, fox_w_in[j], fox_b_f[j], fox_q_norm_g[j], fox_k_norm_g[j], fox_w_out[j])
        x = x + 0.5 * swiglu(rmsnorm(x, norm_g[layer, 2]), ffn_w_in[layer, 1], ffn_w_out[layer, 1])
    return x
```

```python
import os
from contextlib import ExitStack
import numpy as np
import ml_dtypes
import concourse.bass as bass
import concourse.mybir as mybir
from concourse.bass_utils import run_bass_kernel_spmd

F32 = mybir.dt.float32
BF16 = mybir.dt.bfloat16
AF = mybir.ActivationFunctionType
ALU = mybir.AluOpType
AX = mybir.AxisListType

D = 1024
DFF = 2816
NFC = DFF // 128
NT = 4096
TS = 512
SEQ = 16384
EPS = 1e-6
NCORES = 8


class Buf:
    __slots__ = ("writers", "readers", "war", "name")

    def __init__(self, name=""):
        self.writers = []
        self.readers = []
        self.war = []
        self.name = name


class Op:
    __slots__ = ("eng", "fn", "deps", "idx", "eidx", "is_dma", "needs_inc", "sem", "val", "fence")

    def __init__(self, eng, fn, is_dma=False):
        self.eng = eng
        self.fn = fn
        self.deps = []
        self.is_dma = is_dma
        self.needs_inc = False
        self.sem = None
        self.val = 0
        self.fence = False


ENGS = ("pe", "act", "dve", "pool", "sp")
N_DMA_SEMS = 24
SAME_ENG_DIST = 6


class Prog:
    def __init__(self, nc):
        self.nc = nc
        self.ops = []
        self.eng_ops = {e: [] for e in ENGS}
        self.dma_count = {e: 0 for e in ENGS}
        self.all_dmas = []

    def buf(self, name=""):
        return Buf(name)

    def _compress(self, lst, op):
        if not op.is_dma:
            for i, o in enumerate(lst):
                if (not o.is_dma) and o.eng == op.eng:
                    lst[i] = op
                    return
        lst.append(op)

    def _add(self, op, r, w, pw):
        op.idx = len(self.ops)
        op.eidx = len(self.eng_ops[op.eng])
        deps = []
        for b in r:
            deps.extend(b.writers)
        for b in w:
            deps.extend(b.writers)
            deps.extend(b.readers)
            deps.extend(b.war)
        for b in pw:
            if b.readers:
                b.war = b.readers
                b.readers = []
                b.writers = []
            deps.extend(b.war)
        for b in r:
            self._compress(b.readers, op)
        for b in w:
            b.writers = [op]
            b.readers = []
            b.war = []
        for b in pw:
            self._compress(b.writers, op)
        op.deps = deps
        self.ops.append(op)
        self.eng_ops[op.eng].append(op)
        return op

    def op(self, eng, fn, r=(), w=(), pw=()):
        return self._add(Op(eng, fn), r, w, pw)

    def dma(self, out, in_, r=(), w=(), pw=(), eng="sp", **kw):
        def fn(e, out=out, in_=in_, kw=kw):
            return e.dma_start(out=out, in_=in_, **kw)
        op = Op(eng, fn, is_dma=True)
        self._add(op, r, w, pw)
        self.all_dmas.append(op)
        return op

    def fence(self):
        last = []
        for e in ENGS:
            for o in reversed(self.eng_ops[e]):
                if o.fn is not None and not o.is_dma:
                    last.append(o)
                    break
        dmas = list(self.all_dmas[-(N_DMA_SEMS * 2):])
        for e in ENGS:
            op = Op(e, None)
            op.fence = True
            op.idx = len(self.ops)
            op.eidx = len(self.eng_ops[e])
            op.deps = [o for o in last if o.eng != e] + dmas
            self.ops.append(op)
            self.eng_ops[e].append(op)

    def finish(self, final_bufs):
        op = Op("sp", None)
        op.fence = True
        op.idx = len(self.ops)
        op.eidx = len(self.eng_ops["sp"])
        for b in final_bufs:
            op.deps.extend(b.writers)
        op.deps.extend(self.all_dmas[-N_DMA_SEMS:])
        self.ops.append(op)
        self.eng_ops["sp"].append(op)

    def emit(self, stack):
        nc = self.nc
        for op in self.ops:
            for d in op.deps:
                if d.is_dma:
                    continue
                if d.eng == op.eng:
                    if d.eng == "pe":
                        continue
                    if op.eidx - d.eidx > SAME_ENG_DIST:
                        continue
                d.needs_inc = True
        esem = {e: stack.enter_context(nc.semaphore("s_" + e)) for e in ENGS}
        dsem = {}
        for e in ENGS:
            if any(o.is_dma for o in self.eng_ops[e]):
                dsem[e] = [stack.enter_context(nc.semaphore("d_%s_%d" % (e, i))) for i in range(N_DMA_SEMS)]
        cnt = {e: 0 for e in ENGS}
        dcnt = {e: 0 for e in ENGS}
        for op in self.ops:
            if op.is_dma:
                k = dcnt[op.eng]
                dcnt[op.eng] += 1
                op.sem = dsem[op.eng][k % N_DMA_SEMS]
                op.val = 16 * (k // N_DMA_SEMS + 1)
            elif op.needs_inc:
                cnt[op.eng] += 1
                op.sem = esem[op.eng]
                op.val = cnt[op.eng]
        block = stack.enter_context(nc.Block())
        handles = {"pe": block.tensor, "act": block.scalar, "dve": block.vector,
                   "pool": block.gpsimd, "sp": block.sync}

        def make(e):
            ops = self.eng_ops[e]

            def body(eng):
                waited = {}
                for op in ops:
                    need = {}
                    if op.is_dma and op.val > 16:
                        need[id(op.sem)] = (op.sem, op.val - 16)
                    for d in op.deps:
                        if d.sem is None:
                            continue
                        if (not d.is_dma) and d.eng == e and (e == "pe" or op.eidx - d.eidx > SAME_ENG_DIST):
                            continue
                        k = id(d.sem)
                        if k not in need or need[k][1] < d.val:
                            need[k] = (d.sem, d.val)
                    for k, (s, v) in need.items():
                        if waited.get(k, 0) < v:
                            eng.wait_ge(s, v)
                            waited[k] = v
                            self.trace[e].append(("w", k, v))
                    if op.fn is not None and op.sem is not None:
                        self.trace[e].append(("i", id(op.sem), 16 if op.is_dma else 1))
                    if op.fn is None:
                        continue
                    ins = op.fn(eng)
                    if op.is_dma:
                        ins.then_inc(op.sem, 16)
                    elif op.needs_inc:
                        ins.then_inc(op.sem, 1)
            return body

        self.trace = {e: [] for e in ENGS}
        for e in ENGS:
            handles[e](make(e))

    def simulate(self):
        sems = {}
        pc = {e: 0 for e in ENGS}
        while True:
            prog = False
            for e in ENGS:
                t = self.trace[e]
                while pc[e] < len(t):
                    kind, k, v = t[pc[e]]
                    if kind == "w":
                        if sems.get(k, 0) < v:
                            break
                    else:
                        sems[k] = sems.get(k, 0) + v
                    pc[e] += 1
                    prog = True
            if not prog:
                break
        stuck = {e: (pc[e], len(self.trace[e])) for e in ENGS if pc[e] < len(self.trace[e])}
        return stuck


class Ctx:
    pass


_UNIQ = [0]


def sb(stack, nc, name, shape, dt):
    _UNIQ[0] += 1
    return stack.enter_context(nc.sbuf_tensor("%s_u%d" % (name, _UNIQ[0]), list(shape), dt))


def xt_view(x_ap, t0, n):
    return x_ap.rearrange("(kc p) t -> p kc t", p=128)[:, :, t0:t0 + n]


def mm(P, out, lhsT, rhs, start, stop, r, pw):
    P.op("pe", lambda e: e.matmul(out, lhsT, rhs, start=start, stop=stop), r=r, pw=pw)


class WLoader:
    def __init__(self, C, stack, name, n_elems, nslots=2):
        self.C = C
        self.n = n_elems
        self.slots = [(sb(stack, C.nc, "%s_stg%d" % (name, i), [128, n_elems], F32), C.P.buf())
                      for i in range(nslots)]
        self.k = 0

    def load(self, dst_ap, dst_buf, src_ap, shape_fn, scale_ap=None):
        C = self.C
        stg, sbuf_ = self.slots[self.k % len(self.slots)]
        self.k += 1
        n = 1
        for s in src_ap.shape[1:]:
            n *= s
        view = shape_fn(stg[:, 0:n])
        C.P.dma(view, src_ap, w=[sbuf_])
        C.P.op("pool", lambda e: e.tensor_copy(out=dst_ap, in_=view), r=[sbuf_], pw=[dst_buf])


def rmsnorm_tile(C, xt, xbuf, g_ap, hT_slices, hbuf, sq, sqbuf, ps, psbuf, rt, rtbuf, rstd, rstdbuf):
    P = C.P
    P.op("act", lambda e: e.activation(out=sq[:], in_=xt[:], func=AF.Square), r=[xbuf], w=[sqbuf])
    for kc in range(8):
        mm(P, ps[:], C.ones_f32[:], sq[:, kc, :], kc == 0, kc == 7, [sqbuf], [psbuf])
    P.op("act", lambda e: e.activation(out=rt[:], in_=ps[:], func=AF.Sqrt, bias=C.eps_col[:], scale=1.0 / D),
         r=[psbuf], w=[rtbuf])
    P.op("dve", lambda e: e.reciprocal(out=rstd[:], in_=rt[:]), r=[rtbuf], w=[rstdbuf])
    for kc in range(8):
        P.op("dve", lambda e, kc=kc: e.scalar_tensor_tensor(
            out=hT_slices[kc], in0=xt[:, kc, :], scalar=g_ap[:, kc:kc + 1], in1=rstd[:],
            op0=ALU.mult, op1=ALU.mult), r=[xbuf, rstdbuf], pw=[hbuf])


def ffn_stage(C, x_in, x_out, xbufs_in, xbufs_out, w_in, w_out, g_sb, TP=1024):
    nc, P = C.nc, C.P
    nsub = TP // TS
    npass = NT // TP
    with ExitStack() as st:
        NXT = nsub
        xts = [(sb(st, nc, "f_xt%d" % i, [128, 8, TS], F32), P.buf()) for i in range(NXT)]
        sq, sqb = sb(st, nc, "f_sq", [128, 8, TS], F32), P.buf()
        rt, rtb = sb(st, nc, "f_rt", [128, TS], F32), P.buf()
        rstd, rstdb = sb(st, nc, "f_rstd", [128, TS], F32), P.buf()
        hT, hb = sb(st, nc, "f_hT", [128, 8, TP], BF16), [P.buf() for _ in range(nsub)]
        actT = sb(st, nc, "f_actT", [128, NFC, TP], BF16)
        actb = [[P.buf() for _ in range(nsub)] for _ in range(NFC)]
        wi = [(sb(st, nc, "f_wi%d" % i, [128, 8, 2, 128], BF16), P.buf()) for i in range(2)]
        wo, wob = sb(st, nc, "f_wo", [128, NFC, D], BF16), [P.buf() for _ in range(NFC)]
        sg = [(sb(st, nc, "f_sg%d" % i, [128, TS], F32), P.buf()) for i in range(2)]
        ld_in = WLoader(C, st, "f_li", 8 * 2 * 128, 2)
        ld_out = WLoader(C, st, "f_lo", D, 2)
        ps_ss, ps_ssb = C.psum[0], C.psb[0]
        ps_gu = [(C.psum[1 + i], C.psb[1 + i]) for i in range(4)]
        ps_y = [(C.psum[5 + i], C.psb[5 + i]) for i in range(2)]
        w_in_v = w_in.rearrange("(kc p) n -> p kc n", p=128)
        w_out_v = w_out.rearrange("(fc p) n -> p fc n", p=128)
        xi = 0
        gi = 0
        yi = 0
        for ps_ in range(npass):
            t0 = ps_ * TP
            cur = []
            for s in range(nsub):
                xt, xb = xts[xi % NXT]
                xi += 1
                gts = (t0 + s * TS) // TS
                P.dma(xt[:], xt_view(x_in, t0 + s * TS, TS), r=[xbufs_in[gts]], w=[xb])
                cur.append((xt, xb, gts))
                rmsnorm_tile(C, xt, xb, g_sb, [hT[:, kc, s * TS:(s + 1) * TS] for kc in range(8)], hb[s],
                             sq, sqb, ps_ss, ps_ssb, rt, rtb, rstd, rstdb)
            for fc in range(NFC):
                wt, wb = wi[fc % 2]
                for j, c0 in enumerate((fc * 128, DFF + fc * 128)):
                    ld_in.load(wt[:, :, j, :], wb, w_in_v[:, :, c0:c0 + 128],
                               lambda a: a[:, 0:1024].rearrange("p (kc n) -> p kc n", kc=8))
                for s in range(nsub):
                    pg, pgb = ps_gu[gi % 4]
                    pu, pub = ps_gu[(gi + 1) % 4]
                    gi += 2
                    for kc in range(8):
                        mm(P, pg[:], wt[:, kc, 0, :], hT[:, kc, s * TS:(s + 1) * TS], kc == 0, kc == 7,
                           [wb, hb[s]], [pgb])
                    for kc in range(8):
                        mm(P, pu[:], wt[:, kc, 1, :], hT[:, kc, s * TS:(s + 1) * TS], kc == 0, kc == 7,
                           [wb, hb[s]], [pub])
                    sgt, sgb = sg[(fc * nsub + s) % 2]
                    P.op("act", lambda e, sgt=sgt, pg=pg: e.activation(out=sgt[:], in_=pg[:], func=AF.Silu),
                         r=[pgb], w=[sgb])
                    P.op("dve", lambda e, sgt=sgt, pu=pu, fc=fc, s=s: e.tensor_tensor(
                        out=actT[:, fc, s * TS:(s + 1) * TS], in0=pu[:], in1=sgt[:], op=ALU.mult),
                        r=[pub, sgb], w=[actb[fc][s]])
            for fc in range(NFC):
                ld_out.load(wo[:, fc, :], wob[fc], w_out_v[:, fc, :], lambda a: a)
            for s in range(nsub):
                xt, xb, gts = cur[s]
                for dc in range(8):
                    py, pyb = ps_y[yi % 2]
                    yi += 1
                    for fc in range(NFC):
                        mm(P, py[:], wo[:, fc, dc * 128:(dc + 1) * 128], actT[:, fc, s * TS:(s + 1) * TS],
                           fc == 0, fc == NFC - 1, [wob[fc], actb[fc][s]], [pyb])
                    P.op("dve", lambda e, xt=xt, py=py, dc=dc: e.scalar_tensor_tensor(
                        out=xt[:, dc, :], in0=py[:], scalar=0.5, in1=xt[:, dc, :],
                        op0=ALU.mult, op1=ALU.add), r=[pyb], pw=[xb])
                P.dma(xt_view(x_out, t0 + s * TS, TS), xt[:], r=[xb], w=[xbufs_out[gts]])
    P.fence()


def const_arrays():
    c = {}
    c["ones_f32"] = np.ones((128, 128), np.float32)
    c["eps_col"] = np.full((128, 1), EPS, np.float32)
    c["one_col"] = np.ones((128, 1), np.float32)
    c["ident_bf"] = np.eye(128, dtype=np.float32).astype(ml_dtypes.bfloat16)
    c["ident_f32"] = np.eye(128, dtype=np.float32)
    rm = np.ones((128, 2048), np.float32)
    rm[:, ::64] = 0.0
    c["rmask"] = rm
    p = np.arange(128)[:, None]
    f = np.arange(512)[None, :]
    hm = ((p // 64) == ((f // 64) % 2)) & ((p % 64) <= (f % 64))
    c["hmask"] = hm.astype(np.float32).astype(ml_dtypes.bfloat16)
    dm = np.stack([(p + 128 * r <= f) for r in range(4)], 1)
    c["dmask"] = dm.astype(np.float32).astype(ml_dtypes.bfloat16)
    bo = np.zeros((128, 128), np.float32)
    bo[:64, :64] = 1.0
    bo[64:, 64:] = 1.0
    c["blockones"] = bo
    su = (np.arange(128)[:, None] < np.arange(128)[None, :]).astype(np.float32)
    c["sutri"] = su
    return c


CONSTS = const_arrays()


def dt_of(a):
    return BF16 if a.dtype == ml_dtypes.bfloat16 else F32


def build_launch(build_fn, in_specs, out_specs, consts=("ones_f32", "eps_col")):
    nc = bass.Bass("TRN2", target_bir_lowering=False)
    dram = {}
    for n, (shape, dt) in in_specs.items():
        dram[n] = nc.dram_tensor(n, list(shape), dt, kind="ExternalInput").ap()
    for n in consts:
        a = CONSTS[n]
        dram["c_" + n] = nc.dram_tensor("c_" + n, list(a.shape), dt_of(a), kind="ExternalInput").ap()
    for n, (shape, dt) in out_specs.items():
        dram[n] = nc.dram_tensor(n, list(shape), dt, kind="ExternalOutput").ap()
    C = Ctx()
    C.nc = nc
    C.P = Prog(nc)
    with ExitStack() as st:
        C.psum = [st.enter_context(nc.psum_tensor("psb%d" % i, [128, 512], F32)) for i in range(8)]
        C.psb = [C.P.buf() for _ in range(8)]
        for n in consts:
            a = CONSTS[n]
            t = sb(st, nc, "k_" + n, list(a.shape), dt_of(a))
            setattr(C, n, t)
            C.P.dma(t[:], dram["c_" + n], w=[C.P.buf()])
        for i in range(8):
            C.P.op("dve", lambda e, i=i: e.memset(C.psum[i][:], 0.0), w=[C.psb[i]])
        C.P.fence()
        finals = build_fn(C, dram, st)
        C.P.finish(finals)
        C.P.emit(st)
    return nc


def run_launch(build_fn, in_specs, out_specs, in_maps, consts=("ones_f32", "eps_col"), trace=False):
    nc = build_launch(build_fn, in_specs, out_specs, consts)
    for m in in_maps:
        for n in consts:
            m["c_" + n] = CONSTS[n]
    res = run_bass_kernel_spmd(nc, in_maps, core_ids=list(range(NCORES)), **({"trace": True} if trace else {}))
    if os.environ.get("KDEBUG"):
        for n in out_specs:
            bad = [int((~np.isfinite(np.asarray(r[n]).astype(np.float32))).sum()) for r in res.results]
            print("KDEBUG launch out", n, "nonfinite per core:", bad, flush=True)
    return res


def np_dt(dt):
    return {F32: np.float32, BF16: ml_dtypes.bfloat16}[dt]


def load_w_bf16(C, st, name, w_dram, ncols, ld):
    nc, P = C.nc, C.P
    wt = sb(st, nc, name, [128, 8, ncols], BF16)
    wb = P.buf()
    wv = w_dram.rearrange("(kc p) n -> p kc n", p=128)
    CH = 2048
    for kc in range(8):
        for c0 in range(0, ncols, CH):
            n = min(CH, ncols - c0)
            ld.load(wt[:, kc, c0:c0 + n], wb, wv[:, kc, c0:c0 + n], lambda a: a)
    return wt, wb


def load_x_norm(C, x_in, xbufs_in, gts, xt, xb, g_sb, hT, hb, sq, sqb, rt, rtb, rstd, rstdb):
    C.P.dma(xt[:], xt_view(x_in, gts * TS, TS), r=[xbufs_in[gts]], w=[xb])
    rmsnorm_tile(C, xt, xb, g_sb, [hT[:, kc, :] for kc in range(8)], hb,
                 sq, sqb, C.psum[0], C.psb[0], rt, rtb, rstd, rstdb)


def proj_fm(C, ps, psb, wt, wb, c0, m, hT, hb):
    for kc in range(8):
        mm(C.P, ps, wt[:, kc, c0:c0 + m], hT[:, kc, :], kc == 0, kc == 7, [wb, hb], [psb])


def hgrn_a_stage(C, x_in, xbufs_in, w_in, g_sb, lb_sb, oml_sb, qx, kx, vx, sc, gT, obuf):
    nc, P = C.nc, C.P
    HH = 4
    with ExitStack() as st:
        ld = WLoader(C, st, "ha_ld", 2048, 2)
        wt, wb = load_w_bf16(C, st, "ha_w", w_in, 4096, ld)
        xt, xb = sb(st, nc, "ha_xt", [128, 8, TS], F32), P.buf()
        sq, sqb = sb(st, nc, "ha_sq", [128, 8, TS], F32), P.buf()
        rt, rtb = sb(st, nc, "ha_rt", [128, TS], F32), P.buf()
        rstd, rstdb = sb(st, nc, "ha_rstd", [128, TS], F32), P.buf()
        hT, hb = sb(st, nc, "ha_hT", [128, 8, TS], BF16), P.buf()
        qf, qfb = sb(st, nc, "ha_qf", [128, HH, TS], F32), P.buf()
        ff, ffb = sb(st, nc, "ha_f", [128, HH, TS], F32), P.buf()
        kk, kkb = sb(st, nc, "ha_k", [128, HH, TS], F32), P.buf()
        bb, bbb = sb(st, nc, "ha_b", [128, HH, TS], F32), P.buf()
        e1, e1b = sb(st, nc, "ha_e1", [128, HH, TS], F32), P.buf()
        e2, e2b = sb(st, nc, "ha_e2", [128, HH, TS], F32), P.buf()
        qt, qtb = sb(st, nc, "ha_qt", [128, HH, TS], BF16), P.buf()
        kt, ktb = sb(st, nc, "ha_kt", [128, HH, TS], BF16), P.buf()
        gt, gtb = sb(st, nc, "ha_gt", [128, HH, TS], BF16), P.buf()
        vt, vtb = sb(st, nc, "ha_vt", [128, 4, D], BF16), P.buf()
        bm, bmb = sb(st, nc, "ha_bm", [128, HH * 8, 2], F32), P.buf()
        sct, sctb = sb(st, nc, "ha_sc", [128, HH, 3, 8], F32), P.buf()
        tmpc, tmpcb = sb(st, nc, "ha_tmpc", [128, HH * 8], F32), P.buf()
        pi = 0
        banks = [(C.psum[1 + i], C.psb[1 + i]) for i in range(6)]
        for ts_ in range(NT // TS):
            t0 = ts_ * TS
            load_x_norm(C, x_in, xbufs_in, ts_, xt, xb, g_sb, hT, hb, sq, sqb, rt, rtb, rstd, rstdb)
            for tb in range(4):
                for j in range(2):
                    ps, psb = banks[pi % 6]
                    pi += 1
                    for kc in range(8):
                        mm(P, ps[:], hT[:, kc, tb * 128:(tb + 1) * 128], wt[:, kc, 2048 + j * 512:2048 + (j + 1) * 512],
                           kc == 0, kc == 7, [wb, hb], [psb])
                    P.op("dve", lambda e, ps=ps, tb=tb, j=j: e.tensor_copy(out=vt[:, tb, j * 512:(j + 1) * 512], in_=ps[:]),
                         r=[psb], pw=[vtb])
            P.dma(vx[t0:t0 + TS, :].rearrange("(tb p) n -> p tb n", p=128), vt[:], r=[vtb], pw=[obuf])
            for hh in range(8 // HH):
                for i in range(HH):
                    h = hh * HH + i
                    ps, psb = banks[pi % 6]
                    pi += 1
                    proj_fm(C, ps[:], psb, wt, wb, h * 128, 128, hT, hb)
                    P.op("act", lambda e, ps=ps, i=i: e.activation(out=qf[:, i, :], in_=ps[:], func=AF.Silu),
                         r=[psb], pw=[qfb])
                for i in range(HH):
                    h = hh * HH + i
                    ps, psb = banks[pi % 6]
                    pi += 1
                    proj_fm(C, ps[:], psb, wt, wb, 3072 + h * 128, 128, hT, hb)
                    P.op("act", lambda e, ps=ps, i=i: e.activation(out=gt[:, i, :], in_=ps[:], func=AF.Silu),
                         r=[psb], pw=[gtb])
                P.dma(gT.rearrange("(h p) t -> p h t", p=128)[:, hh * HH:(hh + 1) * HH, t0:t0 + TS], gt[:],
                      r=[gtb], pw=[obuf])
                for i in range(HH):
                    h = hh * HH + i
                    ps, psb = banks[pi % 6]
                    pi += 1
                    proj_fm(C, ps[:], psb, wt, wb, 1024 + h * 128, 128, hT, hb)
                    P.op("act", lambda e, ps=ps, i=i: e.activation(out=ff[:, i, :], in_=ps[:], func=AF.Sigmoid),
                         r=[psb], pw=[ffb])
                    P.op("dve", lambda e, i=i, h=h: e.tensor_scalar(
                        out=ff[:, i, :], in0=ff[:, i, :], scalar1=oml_sb[:, h:h + 1], scalar2=lb_sb[:, h:h + 1],
                        op0=ALU.mult, op1=ALU.add), r=[ffb], pw=[ffb])
                P.op("dve", lambda e: e.tensor_scalar(out=kk[:], in0=ff[:], scalar1=-1.0, scalar2=1.0,
                                                      op0=ALU.mult, op1=ALU.add), r=[ffb], w=[kkb])
                P.op("dve", lambda e: e.tensor_scalar(out=ff[:], in0=ff[:], scalar1=1e-30, scalar2=None,
                                                      op0=ALU.max), r=[ffb], w=[ffb])
                P.op("act", lambda e: e.activation(out=ff[:], in_=ff[:], func=AF.Ln), r=[ffb], w=[ffb])
                ffl = ff[:].rearrange("p h t -> p (h t)")
                bbl = bb[:].rearrange("p h t -> p (h t)")
                P.op("dve", lambda e: e.tensor_tensor_scan(out=bbl, data0=C.rmask[:, 0:HH * TS], data1=ffl, initial=0.0,
                                                           op0=ALU.mult, op1=ALU.add), r=[ffb], w=[bbb])
                bc = bb[:].rearrange("p h (c s) -> p (h c) s", s=64)
                P.op("dve", lambda e: e.tensor_copy(out=bm[:, :, 0], in_=bc[:, :, 31]), r=[bbb], pw=[bmb])
                P.op("dve", lambda e: e.tensor_copy(out=bm[:, :, 1], in_=bc[:, :, 63]), r=[bbb], pw=[bmb])
                P.op("dve", lambda e: e.tensor_tensor(out=bc, in0=bc, in1=bm[:, :, 0:1].broadcast_to([128, HH * 8, 64]),
                                                      op=ALU.subtract), r=[bbb, bmb], w=[bbb])
                P.op("act", lambda e: e.activation(out=e1[:], in_=bb[:], func=AF.Exp), r=[bbb], w=[e1b])
                P.op("act", lambda e: e.activation(out=e2[:], in_=bb[:], func=AF.Exp, scale=-1.0), r=[bbb], w=[e2b])
                sv = sct[:].rearrange("p h k c -> p k h c")
                bmv = bm[:].rearrange("p (h c) k -> p k h c", c=8)
                P.op("act", lambda e: e.activation(out=sv[:, 0], in_=bmv[:, 1], func=AF.Exp), r=[bmb], pw=[sctb])
                P.op("act", lambda e: e.activation(out=sv[:, 1], in_=bmv[:, 0], func=AF.Exp), r=[bmb], pw=[sctb])
                P.op("dve", lambda e: e.tensor_tensor(out=tmpc[:], in0=bm[:, :, 1], in1=bm[:, :, 0], op=ALU.subtract),
                     r=[bmb], w=[tmpcb])
                P.op("act", lambda e: e.activation(out=sv[:, 2], in_=tmpc[:].rearrange("p (h c) -> p h c", c=8),
                                                   func=AF.Exp), r=[tmpcb], pw=[sctb])
                P.op("dve", lambda e: e.tensor_tensor(out=qt[:], in0=qf[:], in1=e1[:], op=ALU.mult),
                     r=[qfb, e1b], w=[qtb])
                P.op("dve", lambda e: e.tensor_tensor(out=kt[:], in0=kk[:], in1=e2[:], op=ALU.mult),
                     r=[kkb, e2b], w=[ktb])
                hs = slice(hh * HH, (hh + 1) * HH)
                P.dma(qx[hs, :, t0:t0 + TS].rearrange("h p t -> p h t"), qt[:], r=[qtb], pw=[obuf])
                P.dma(kx[hs, :, t0:t0 + TS].rearrange("h p t -> p h t"), kt[:], r=[ktb], pw=[obuf])
                for i in range(HH):
                    P.dma(sc[hh * HH + i, :, :, ts_ * 8:(ts_ + 1) * 8], sct[:, i], r=[sctb], pw=[obuf])
    P.fence()


def hgrn_b_stage(C, qx2, kx2, vx2, sc2, on_g, oT2, ibuf, obuf, NPR=2):
    nc, P = C.nc, C.P
    with ExitStack() as st:
        scs = sb(st, nc, "hb_sc", [128, NPR, 3, SEQ // 64], F32)
        scb = P.buf()
        P.dma(scs[:], sc2.rearrange("r p k c -> p r k c"), r=[ibuf], w=[scb])
        S = [(sb(st, nc, "hb_S%d" % r, [128, 128], F32), P.buf()) for r in range(NPR)]
        Sd = [(sb(st, nc, "hb_Sd%d" % r, [128, 128], F32), P.buf()) for r in range(NPR)]
        Sb = [(sb(st, nc, "hb_Sb%d" % r, [128, 128], BF16), P.buf()) for r in range(NPR)]
        NB = 2
        qts = [[(sb(st, nc, "hb_q%d_%d" % (r, i), [128, TS], BF16), P.buf()) for i in range(NB)] for r in range(NPR)]
        kts = [[(sb(st, nc, "hb_k%d_%d" % (r, i), [128, TS], BF16), P.buf()) for i in range(NB)] for r in range(NPR)]
        vts = [[(sb(st, nc, "hb_v%d_%d" % (r, i), [128, 4, 128], BF16), P.buf()) for i in range(NB)] for r in range(NPR)]
        ktok = [(sb(st, nc, "hb_kt%d" % r, [128, 4, 128], BF16), P.buf()) for r in range(NPR)]
        sm = [(sb(st, nc, "hb_sm%d" % r, [128, TS], BF16), P.buf()) for r in range(NPR)]
        sq, sqb = sb(st, nc, "hb_sq", [128, TS], F32), P.buf()
        rt, rtb = sb(st, nc, "hb_rt", [128, TS], F32), P.buf()
        rstd, rstdb = sb(st, nc, "hb_rstd", [128, TS], F32), P.buf()
        on = [(sb(st, nc, "hb_on%d" % i, [128, TS], F32), P.buf()) for i in range(2)]
        for r in range(NPR):
            P.op("dve", lambda e, r=r: e.memset(S[r][0][:], 0.0), w=[S[r][1]])
            P.op("dve", lambda e, r=r: e.memset(Sb[r][0][:], 0.0), w=[Sb[r][1]])
        psT, psTb = C.psum[0], C.psb[0]
        psS, psSb = C.psum[1], C.psb[1]
        po = [(C.psum[2 + r], C.psb[2 + r]) for r in range(2)]
        psU = [(C.psum[4 + r], C.psb[4 + r]) for r in range(2)]
        pss, pssb = C.psum[6], C.psb[6]
        psT_bf = psT[:].bitcast(BF16)
        oi = 0
        ntile = SEQ // TS

        def issue_loads(tt):
            for r in range(NPR):
                q_, qb_ = qts[r][tt % NB]
                k_, kb_ = kts[r][tt % NB]
                v_, vb_ = vts[r][tt % NB]
                P.dma(q_[:], qx2[r, :, tt * TS:(tt + 1) * TS], r=[ibuf], w=[qb_])
                P.dma(k_[:], kx2[r, :, tt * TS:(tt + 1) * TS], r=[ibuf], w=[kb_])
                P.dma(v_[:], vx2[r, tt * TS:(tt + 1) * TS, :].rearrange("(b p) v -> p b v", p=128), r=[ibuf], w=[vb_])

        issue_loads(0)
        for tt in range(ntile):
            if tt + 1 < ntile:
                issue_loads(tt + 1)
            for r in range(NPR):
                q_, qb_ = qts[r][tt % NB]
                k_, kb_ = kts[r][tt % NB]
                v_, vb_ = vts[r][tt % NB]
                kt_, ktb_ = ktok[r]
                sm_, smb_ = sm[r]
                S_, Sbuf_ = S[r]
                Sd_, Sdb_ = Sd[r]
                Sb_, Sbb_ = Sb[r]
                po_, pob_ = po[r]
                pu_, pub_ = psU[r]
                for b in range(4):
                    P.op("pe", lambda e, b=b, k_=k_: e.transpose(out=psT_bf[:, b * 128:(b + 1) * 128],
                                                                  in_=k_[:, b * 128:(b + 1) * 128], identity=C.ident_bf[:]),
                         r=[kb_], pw=[psTb])
                P.op("act", lambda e, kt_=kt_: e.activation(out=kt_[:].rearrange("p b k -> p (b k)"), in_=psT_bf[:, 0:512],
                                                            func=AF.Copy), r=[psTb], w=[ktb_])
                for c in range(8):
                    hf = c % 2
                    P.op("pe", lambda e, c=c, hf=hf, k_=k_, q_=q_: e.matmul(
                        psS[hf * 64:(hf + 1) * 64, c * 64:(c + 1) * 64], k_[:, c * 64:(c + 1) * 64], q_[:, c * 64:(c + 1) * 64],
                        start=True, stop=True), r=[kb_, qb_], pw=[psSb])
                P.op("dve", lambda e, sm_=sm_: e.tensor_tensor(out=sm_[:], in0=psS[:], in1=C.hmask[:], op=ALU.mult),
                     r=[psSb], w=[smb_])
                for c in range(8):
                    hf = c % 2
                    b = c // 2
                    gc = tt * 8 + c
                    cs = slice(c * 64, (c + 1) * 64)
                    ps_ = slice(hf * 64, (hf + 1) * 64)
                    P.op("pe", lambda e, v_=v_, sm_=sm_, po_=po_, ps_=ps_, b=b, cs=cs: e.matmul(
                        po_[:, cs], v_[ps_, b, :], sm_[ps_, cs], start=True, stop=False),
                        r=[vb_, smb_], pw=[pob_])
                    P.op("pe", lambda e, Sb_=Sb_, q_=q_, po_=po_, cs=cs: e.matmul(
                        po_[:, cs], Sb_[:], q_[:, cs], start=False, stop=True),
                        r=[Sbb_, qb_], pw=[pob_])
                    P.op("pe", lambda e, kt_=kt_, v_=v_, pu_=pu_, ps_=ps_, b=b: e.matmul(
                        pu_[:, 0:128], kt_[ps_, b, :], v_[ps_, b, :], start=True, stop=True),
                        r=[ktb_, vb_], w=[pub_])
                    P.op("dve", lambda e, Sd_=Sd_, S_=S_, r=r, gc=gc: e.tensor_scalar(
                        out=Sd_[:], in0=S_[:], scalar1=scs[:, r, 0, gc:gc + 1], scalar2=None, op0=ALU.mult),
                        r=[Sbuf_, scb], w=[Sdb_])
                    P.op("dve", lambda e, Sd_=Sd_, S_=S_, pu_=pu_, r=r, gc=gc: e.scalar_tensor_tensor(
                        out=S_[:], in0=pu_[:, 0:128], scalar=scs[:, r, 2, gc:gc + 1], in1=Sd_[:],
                        op0=ALU.mult, op1=ALU.add), r=[pub_, Sdb_, scb], w=[Sbuf_])
                    if gc + 1 < SEQ // 64:
                        P.op("dve", lambda e, Sb_=Sb_, S_=S_, r=r, gc=gc: e.tensor_scalar(
                            out=Sb_[:], in0=S_[:], scalar1=scs[:, r, 1, gc + 1:gc + 2], scalar2=None, op0=ALU.mult),
                            r=[Sbuf_, scb], w=[Sbb_])
                P.op("act", lambda e, po_=po_: e.activation(out=sq[:], in_=po_[:], func=AF.Square), r=[pob_], w=[sqb])
                mm(P, pss[:], C.ones_f32[:], sq[:], True, True, [sqb], [pssb])
                P.op("act", lambda e: e.activation(out=rt[:], in_=pss[:], func=AF.Sqrt, bias=C.eps_col[:], scale=1.0 / 128),
                     r=[pssb], w=[rtb])
                P.op("dve", lambda e: e.reciprocal(out=rstd[:], in_=rt[:]), r=[rtb], w=[rstdb])
                o_, ob_ = on[oi % 2]
                oi += 1
                P.op("dve", lambda e, o_=o_, po_=po_: e.scalar_tensor_tensor(
                    out=o_[:], in0=po_[:], scalar=on_g[:, 0:1], in1=rstd[:], op0=ALU.mult, op1=ALU.mult),
                    r=[pob_, rstdb], w=[ob_])
                P.dma(oT2[r, :, tt * TS:(tt + 1) * TS], o_[:], r=[ob_], pw=[obuf])
    P.fence()


def mix_c_stage(C, x_in, x_out, xbufs_in, xbufs_out, oT, gT, w_out, ibuf):
    nc, P = C.nc, C.P
    with ExitStack() as st:
        ld = WLoader(C, st, "mc_ld", 1024, 2)
        wt, wb = load_w_bf16(C, st, "mc_w", w_out, D, ld)
        NB = 2
        xts = [(sb(st, nc, "mc_xt%d" % i, [128, 8, TS], F32), P.buf()) for i in range(NB)]
        ots = [(sb(st, nc, "mc_ot%d" % i, [128, 8, TS], F32), P.buf()) for i in range(NB)]
        gts_ = [(sb(st, nc, "mc_gt%d" % i, [128, 8, TS], BF16), P.buf()) for i in range(NB)]
        og, ogb = sb(st, nc, "mc_og", [128, 8, TS], BF16), P.buf()
        yi = 0
        for ts_ in range(NT // TS):
            t0 = ts_ * TS
            xt, xb = xts[ts_ % NB]
            ot, otb = ots[ts_ % NB]
            gt, gtb = gts_[ts_ % NB]
            P.dma(xt[:], xt_view(x_in, t0, TS), r=[xbufs_in[ts_]], w=[xb])
            P.dma(ot[:], xt_view(oT, t0, TS), r=[ibuf], w=[otb])
            P.dma(gt[:], xt_view(gT, t0, TS), r=[ibuf], w=[gtb])
            P.op("dve", lambda e, ot=ot, gt=gt: e.tensor_tensor(out=og[:], in0=ot[:], in1=gt[:], op=ALU.mult),
                 r=[otb, gtb], w=[ogb])
            for dc in range(8):
                py, pyb = C.psum[1 + yi % 4], C.psb[1 + yi % 4]
                yi += 1
                for kc in range(8):
                    mm(P, py[:], wt[:, kc, dc * 128:(dc + 1) * 128], og[:, kc, :], kc == 0, kc == 7, [wb, ogb], [pyb])
                P.op("dve", lambda e, xt=xt, py=py, dc=dc: e.tensor_tensor(
                    out=xt[:, dc, :], in0=py[:], in1=xt[:, dc, :], op=ALU.add), r=[pyb], pw=[xb])
            P.dma(xt_view(x_out, t0, TS), xt[:], r=[xb], w=[xbufs_out[ts_]])
    P.fence()


def fox_a_stage(C, x_in, xbufs_in, w_in, g_sb, gq2, gk2, nbf, qx, kx, vx, gT, lfx, obuf):
    nc, P = C.nc, C.P
    with ExitStack() as st:
        ld = WLoader(C, st, "fa_ld", 2048, 2)
        wt, wb = load_w_bf16(C, st, "fa_w", w_in, 4 * D + 16, ld)
        xt, xb = sb(st, nc, "fa_xt", [128, 8, TS], F32), P.buf()
        sq, sqb = sb(st, nc, "fa_sq", [128, 8, TS], F32), P.buf()
        rt, rtb = sb(st, nc, "fa_rt", [128, TS], F32), P.buf()
        rstd, rstdb = sb(st, nc, "fa_rstd", [128, TS], F32), P.buf()
        hT, hb = sb(st, nc, "fa_hT", [128, 8, TS], BF16), P.buf()
        qraw = [(sb(st, nc, "fa_qr%d" % i, [128, TS], F32), P.buf()) for i in range(2)]
        sq2 = [(sb(st, nc, "fa_s2%d" % i, [128, TS], F32), P.buf()) for i in range(2)]
        rt2 = [(sb(st, nc, "fa_r2%d" % i, [128, TS], F32), P.buf()) for i in range(2)]
        rs2 = [(sb(st, nc, "fa_rs%d" % i, [128, TS], F32), P.buf()) for i in range(2)]
        qt, qtb = sb(st, nc, "fa_qt", [128, 8, TS], BF16), P.buf()
        kt, ktb = sb(st, nc, "fa_kt", [128, 8, TS], BF16), P.buf()
        gt, gtb = sb(st, nc, "fa_gt", [128, 8, TS], BF16), P.buf()
        vt, vtb = sb(st, nc, "fa_vt", [128, 4, D], BF16), P.buf()
        ee, eeb = sb(st, nc, "fa_e", [16, TS], F32), P.buf()
        lft, lfb = sb(st, nc, "fa_lf", [16, TS], F32), P.buf()
        banks = [(C.psum[1 + i], C.psb[1 + i]) for i in range(4)]
        bss = [(C.psum[5 + i], C.psb[5 + i]) for i in range(2)]
        pz, pzb = C.psum[7], C.psb[7]
        pi = 0
        ni = 0
        for ts_ in range(NT // TS):
            t0 = ts_ * TS
            load_x_norm(C, x_in, xbufs_in, ts_, xt, xb, g_sb, hT, hb, sq, sqb, rt, rtb, rstd, rstdb)
            for tb in range(4):
                for j in range(2):
                    ps, psb = banks[pi % 4]
                    pi += 1
                    for kc in range(8):
                        mm(P, ps[:], hT[:, kc, tb * 128:(tb + 1) * 128], wt[:, kc, 2048 + j * 512:2048 + (j + 1) * 512],
                           kc == 0, kc == 7, [wb, hb], [psb])
                    P.op("dve", lambda e, ps=ps, tb=tb, j=j: e.tensor_copy(out=vt[:, tb, j * 512:(j + 1) * 512], in_=ps[:]),
                         r=[psb], pw=[vtb])
            P.dma(vx[t0:t0 + TS, :].rearrange("(tb p) n -> p tb n", p=128), vt[:], r=[vtb], pw=[obuf])
            for which, (dst, dstb, gsc, c0) in enumerate(((qt, qtb, gq2, 0), (kt, ktb, gk2, 1024))):
                for kc2 in range(8):
                    ps, psb = banks[pi % 4]
                    pi += 1
                    proj_fm(C, ps[:], psb, wt, wb, c0 + kc2 * 128, 128, hT, hb)
                    qr, qrb = qraw[ni % 2]
                    s2, s2b = sq2[ni % 2]
                    r2, r2b = rt2[ni % 2]
                    rs, rsb = rs2[ni % 2]
                    pss, pssb = bss[ni % 2]
                    ni += 1
                    P.op("act", lambda e, qr=qr, ps=ps: e.activation(out=qr[:], in_=ps[:], func=AF.Copy), r=[psb], w=[qrb])
                    P.op("pool", lambda e, qr=qr, s2=s2: e.tensor_tensor(out=s2[:], in0=qr[:], in1=qr[:], op=ALU.mult),
                         r=[qrb], w=[s2b])
                    mm(P, pss[:], C.blockones[:], s2[:], True, True, [s2b], [pssb])
                    P.op("act", lambda e, r2=r2, pss=pss: e.activation(out=r2[:], in_=pss[:], func=AF.Sqrt, bias=C.eps_col[:],
                                                                       scale=1.0 / 64), r=[pssb], w=[r2b])
                    P.op("dve", lambda e, rs=rs, r2=r2: e.reciprocal(out=rs[:], in_=r2[:]), r=[r2b], w=[rsb])
                    P.op("dve", lambda e, dst=dst, kc2=kc2, qr=qr, gsc=gsc, rs=rs: e.scalar_tensor_tensor(
                        out=dst[:, kc2, :], in0=qr[:], scalar=gsc[:, 0:1], in1=rs[:], op0=ALU.mult, op1=ALU.mult),
                        r=[qrb, rsb], pw=[dstb])
            P.dma(xt_view(qx, t0, TS), qt[:], r=[qtb], pw=[obuf])
            P.dma(xt_view(kx, t0, TS), kt[:], r=[ktb], pw=[obuf])
            for kc2 in range(8):
                ps, psb = banks[pi % 4]
                pi += 1
                proj_fm(C, ps[:], psb, wt, wb, 3072 + kc2 * 128, 128, hT, hb)
                P.op("act", lambda e, ps=ps, kc2=kc2: e.activation(out=gt[:, kc2, :], in_=ps[:], func=AF.Sigmoid),
                     r=[psb], pw=[gtb])
            P.dma(xt_view(gT, t0, TS), gt[:], r=[gtb], pw=[obuf])
            proj_fm(C, pz[0:16, :], pzb, wt, wb, 4096, 16, hT, hb)
            P.op("act", lambda e: e.activation(out=ee[:], in_=pz[0:16, :], func=AF.Exp, bias=nbf[:, 0:1], scale=-1.0),
                 r=[pzb], w=[eeb])
            P.op("act", lambda e: e.activation(out=ee[:], in_=ee[:], func=AF.Ln, bias=C.one_col[0:16, :], scale=1.0),
                 r=[eeb], w=[eeb])
            P.op("dve", lambda e: e.tensor_scalar(out=lft[:], in0=ee[:], scalar1=-1.0, scalar2=None, op0=ALU.mult),
                 r=[eeb], w=[lfb])
            P.dma(lfx[:, t0:t0 + TS], lft[:], r=[lfb], pw=[obuf])
    P.fence()


def fox_b_stage(C, qx4, kx4, vx4, lf4, negm, cq, oT4, ibuf, obuf, NPR=4, LA=2):
    nc, P = C.nc, C.P
    NBLK = SEQ // 128
    with ExitStack() as st:
        KT, KTb = sb(st, nc, "fb_KT", [67, SEQ], BF16), P.buf()
        V2, V2b = sb(st, nc, "fb_V2", [128, NBLK, 128], BF16), P.buf()
        L, Lb = sb(st, nc, "fb_L", [128, 128], F32), P.buf()
        Cw, Cwb = sb(st, nc, "fb_Cw", [128, 128], F32), P.buf()
        crow, crb = sb(st, nc, "fb_crow", [128, 128], F32), P.buf()
        r1, r1b = sb(st, nc, "fb_r1", [128, 128], F32), P.buf()
        ch, chb = sb(st, nc, "fb_ch", [128, 3, 128], BF16), P.buf()
        biasc, bib = sb(st, nc, "fb_bias", [128, 128], F32), P.buf()
        QT = [(sb(st, nc, "fb_QT%d" % i, [67, TS], BF16), P.buf()) for i in range(2)]
        NPT = 4
        PT = [(sb(st, nc, "fb_PT%d" % i, [128, TS], BF16), P.buf()) for i in range(NPT)]
        R, Rb_ = sb(st, nc, "fb_R", [128, TS], F32), P.buf()
        CL = [(sb(st, nc, "fb_cl%d" % i, [128, TS], F32), P.buf()) for i in range(2)]
        Rc, Rcb = sb(st, nc, "fb_Rc", [64, TS], F32), P.buf()
        oo = [(sb(st, nc, "fb_o%d" % i, [64, TS], F32), P.buf()) for i in range(2)]
        P.op("dve", lambda e: e.memset(KT[64:67, :], 1.0), pw=[KTb])
        P.op("pool", lambda e: e.memset(V2[:, :, 64:128], 1.0), pw=[V2b])
        P.fence()
        pS = [(C.psum[i], C.psb[i]) for i in range(4)]
        pO = [(C.psum[4 + i], C.psb[4 + i]) for i in range(2)]
        pB, pBb = C.psum[6], C.psb[6]
        pM, pMb = C.psum[7], C.psb[7]
        cqb = P.buf()
        si = 0
        pti = 0
        qi = 0
        for r in range(NPR):
            for q4 in range(4):
                P.dma(KT[0:64, q4 * 4096:(q4 + 1) * 4096], kx4[r, :, q4 * 4096:(q4 + 1) * 4096], r=[ibuf], pw=[KTb])
            vv = vx4[r].rearrange("(n p) d -> p n d", p=128)
            for n8 in range(8):
                P.dma(V2[:, n8 * 16:(n8 + 1) * 16, 0:64], vv[:, n8 * 16:(n8 + 1) * 16, :], r=[ibuf], pw=[V2b])
            P.dma(L[:], lf4[r].rearrange("(n p) -> n p", p=128), r=[ibuf], w=[Lb])
            P.op("dve", lambda e: e.tensor_tensor_scan(out=Cw[:], data0=C.ones_f32[:], data1=L[:], initial=0.0,
                                                       op0=ALU.mult, op1=ALU.add), r=[Lb], w=[Cwb])
            mm(P, pM[:, 0:1], C.sutri[:], Cw[:, 127:128], True, True, [Cwb], [pMb])
            P.op("dve", lambda e: e.tensor_scalar(out=crow[:], in0=Cw[:], scalar1=pM[:, 0:1], scalar2=None, op0=ALU.add),
                 r=[Cwb, pMb], w=[crb])
            P.op("dve", lambda e: e.tensor_copy(out=ch[:, 0, :], in_=crow[:]), r=[crb], pw=[chb])
            P.op("dve", lambda e: e.tensor_tensor(out=r1[:], in0=crow[:], in1=ch[:, 0, :], op=ALU.subtract),
                 r=[crb, chb], w=[r1b])
            P.op("dve", lambda e: e.tensor_copy(out=ch[:, 1, :], in_=r1[:]), r=[r1b], pw=[chb])
            P.op("dve", lambda e: e.tensor_tensor(out=r1[:], in0=r1[:], in1=ch[:, 1, :], op=ALU.subtract),
                 r=[r1b, chb], w=[r1b])
            P.op("dve", lambda e: e.tensor_copy(out=ch[:, 2, :], in_=r1[:]), r=[r1b], pw=[chb])
            P.dma(cq[r].rearrange("k (n p) -> n k p", p=128), ch[:], r=[chb], w=[cqb])
            P.op("pe", lambda e: e.transpose(out=pM[:, 128:256], in_=crow[:], identity=C.ident_f32[:]), r=[crb], w=[pMb])
            P.op("dve", lambda e: e.tensor_scalar(out=biasc[:], in0=pM[:, 128:256], scalar1=-1.0, scalar2=negm[:, 0:1],
                                                  op0=ALU.mult, op1=ALU.add), r=[pMb], w=[bib])
            for j in range(SEQ // TS):
                qT, qTb = QT[qi % 2]
                qi += 1
                P.dma(qT[0:64, :], qx4[r, :, j * TS:(j + 1) * TS], r=[ibuf], pw=[qTb])
                P.dma(qT[64:67, :], cq[r, :, j * TS:(j + 1) * TS], r=[cqb], pw=[qTb])
                po, pob = pO[j % 2]
                nkb = 4 * j + 4
                units = []

                def s_mm(i):
                    ps, psb = pS[(si + i) % 4]
                    P.op("pe", lambda e, ps=ps, i=i, qT=qT: e.matmul(ps[:], KT[:, i * 128:(i + 1) * 128], qT[:],
                                                                       start=True, stop=True), r=[KTb, qTb], w=[psb])

                def pv(i):
                    ps, psb = pS[(si + i) % 4]
                    pt, ptb = PT[(pti + i) % NPT]
                    if i >= 4 * j:
                        cl, clb = CL[i % 2]
                        P.op("dve", lambda e, cl=cl, ps=ps, i=i: e.tensor_scalar(
                            out=cl[:], in0=ps[:], scalar1=biasc[:, i:i + 1], scalar2=0.0, op0=ALU.add, op1=ALU.min),
                            r=[psb, bib], w=[clb])
                        P.op("act", lambda e, pt=pt, cl=cl: e.activation(out=pt[:], in_=cl[:], func=AF.Exp),
                             r=[clb], w=[ptb])
                    else:
                        P.op("act", lambda e, pt=pt, ps=ps, i=i: e.activation(out=pt[:], in_=ps[:], func=AF.Exp,
                                                                               bias=biasc[:, i:i + 1], scale=1.0),
                             r=[psb, bib], w=[ptb])
                    if i >= 4 * j:
                        rr = i - 4 * j
                        P.op("dve", lambda e, pt=pt, rr=rr: e.tensor_tensor(out=pt[:], in0=pt[:], in1=C.dmask[:, rr, :],
                                                                           op=ALU.mult), r=[ptb], w=[ptb])
                    P.op("pe", lambda e, pt=pt, i=i, po=po: e.matmul(po[:], V2[:, i, :], pt[:], start=(i == 0),
                                                                      stop=(i == nkb - 1)), r=[V2b, ptb], pw=[pob])

                for i in range(min(LA, nkb)):
                    s_mm(i)
                for i in range(nkb):
                    if i + LA < nkb:
                        s_mm(i + LA)
                    pv(i)
                si += nkb
                pti += nkb
                P.op("dve", lambda e, po=po: e.reciprocal(out=R[64:128, :], in_=po[64:128, :]), r=[pob], w=[Rb_])
                P.op("pe", lambda e: e.matmul(pB[0:64, :], C.ident_f32[64:128, 64:128], R[64:128, :], start=True, stop=True),
                     r=[Rb_], w=[pBb])
                P.op("act", lambda e: e.activation(out=Rc[:], in_=pB[0:64, :], func=AF.Copy), r=[pBb], w=[Rcb])
                o_, ob_ = oo[j % 2]
                P.op("dve", lambda e, o_=o_, po=po: e.tensor_tensor(out=o_[:], in0=po[0:64, :], in1=Rc[:], op=ALU.mult),
                     r=[pob, Rcb], w=[ob_])
                P.dma(oT4[r, :, j * TS:(j + 1) * TS], o_[:], r=[ob_], pw=[obuf])
    P.fence()


def load_gain(C, st, ap1d, name):
    t = sb(st, C.nc, "s_" + name, [128, 8], F32)
    C.P.dma(t[:], ap1d.rearrange("(kc p) -> p kc", p=128), w=[C.P.buf()], allow_slow_non_contiguous=True)
    return t


def hgrn_small(C, dram, st, j):
    nc, P = C.nc, C.P
    lb_sb = sb(st, nc, "lb_sb", [128, 8], F32)
    oml_sb = sb(st, nc, "oml_sb", [128, 8], F32)
    b = P.buf()
    if j == 0:
        P.op("dve", lambda e: e.memset(lb_sb[:], 0.0), w=[b])
    else:
        l0 = load_gain(C, st, dram["lbl"][0], "l0")
        l1 = load_gain(C, st, dram["lbl"][1], "l1")
        P.fence()
        P.op("dve", lambda e: e.tensor_tensor(out=l1[:], in0=l1[:], in1=l0[:], op=ALU.subtract), w=[b])
        P.fence()
        P.op("act", lambda e: e.activation(out=lb_sb[:], in_=l1[:], func=AF.Sigmoid), w=[b])
    P.fence()
    P.op("dve", lambda e: e.tensor_scalar(out=oml_sb[:], in0=lb_sb[:], scalar1=-1.0, scalar2=1.0,
                                          op0=ALU.mult, op1=ALU.add), w=[b])
    P.fence()
    return lb_sb, oml_sb


def fox_small(C, dram, st):
    nc, P = C.nc, C.P
    gq2 = sb(st, nc, "gq2", [128, 1], F32)
    gk2 = sb(st, nc, "gk2", [128, 1], F32)
    nbf = sb(st, nc, "nbf", [16, 1], F32)
    b = P.buf()
    P.dma(gq2[:], dram["gq_col"], w=[P.buf()])
    P.dma(gk2[:], dram["gk_col"], w=[P.buf()])
    P.dma(nbf[:], dram["bf_col"], w=[P.buf()])
    P.fence()
    P.op("dve", lambda e: e.tensor_scalar(out=gq2[:], in0=gq2[:], scalar1=0.125, scalar2=None, op0=ALU.mult), w=[b])
    P.op("dve", lambda e: e.tensor_scalar(out=nbf[:], in0=nbf[:], scalar1=-1.0, scalar2=None, op0=ALU.mult), w=[b])
    P.fence()
    return gq2, gk2, nbf


def fox_negm(C, dram, st):
    nc, P = C.nc, C.P
    gqr = sb(st, nc, "gqr", [128, 64], F32)
    gkr = sb(st, nc, "gkr", [128, 64], F32)
    t1 = sb(st, nc, "nm_t1", [128, 64], F32)
    t2 = sb(st, nc, "nm_t2", [128, 64], F32)
    mq = sb(st, nc, "mq", [128, 1], F32)
    mk = sb(st, nc, "mk", [128, 1], F32)
    negm = sb(st, nc, "negm", [128, 1], F32)
    b = P.buf()
    P.dma(gqr[:], dram["gq_rep"], w=[P.buf()])
    P.dma(gkr[:], dram["gk_rep"], w=[P.buf()])
    P.fence()
    P.op("dve", lambda e: e.tensor_scalar(out=t1[:], in0=gqr[:], scalar1=-1.0, scalar2=None, op0=ALU.mult), w=[b])
    P.op("dve", lambda e: e.tensor_scalar(out=t2[:], in0=gkr[:], scalar1=-1.0, scalar2=None, op0=ALU.mult), w=[b])
    P.fence()
    P.op("dve", lambda e: e.tensor_tensor(out=gqr[:], in0=gqr[:], in1=t1[:], op=ALU.max), w=[b])
    P.op("dve", lambda e: e.tensor_tensor(out=gkr[:], in0=gkr[:], in1=t2[:], op=ALU.max), w=[b])
    P.fence()
    P.op("dve", lambda e: e.tensor_reduce(out=mq[:], in_=gqr[:], axis=AX.X, op=ALU.max), w=[b])
    P.op("dve", lambda e: e.tensor_reduce(out=mk[:], in_=gkr[:], axis=AX.X, op=ALU.max), w=[b])
    P.fence()
    P.op("dve", lambda e: e.scalar_tensor_tensor(out=negm[:], in0=mq[:], scalar=-8.0, in1=mk[:],
                                                 op0=ALU.mult, op1=ALU.mult), w=[b])
    P.fence()
    return negm


XS = ((D, NT), F32)
HG_A_OUT = {"qx": ((8, 128, NT), BF16), "kx": ((8, 128, NT), BF16), "vx": ((NT, D), BF16),
            "sc": ((8, 128, 3, 64), F32), "gT": ((D, NT), BF16)}
FOX_A_OUT = {"qx": ((D, NT), BF16), "kx": ((D, NT), BF16), "vx": ((NT, D), BF16),
             "gT": ((D, NT), BF16), "lfx": ((16, NT), F32)}


def token_launch(p, xT, layer, prev_mix, next_mix, oT=None, gT=None):
    ins = {"xi": XS}
    maps = [{"xi": xT[c]} for c in range(NCORES)]
    consts = ["ones_f32", "eps_col"]

    def add(name, arr):
        arr = np.ascontiguousarray(arr)
        ins[name] = (arr.shape, dt_of(arr))
        for m in maps:
            m[name] = arr

    nl = layer if prev_mix is None else layer + 1
    if prev_mix is not None:
        jm = layer // 2
        ins["oT"] = XS
        ins["gTi"] = ((D, NT), BF16)
        for c in range(NCORES):
            maps[c]["oT"] = oT[c]
            maps[c]["gTi"] = gT[c]
        add("wmo", p["hg_w_out"][jm] if prev_mix == "hg" else p["fox_w_out"][jm])
        add("fb_wi", p["ffn_w_in"][layer, 1])
        add("fb_wo", p["ffn_w_out"][layer, 1])
        add("g_b", p["norm_g"][layer, 2])
    if next_mix is not None:
        jn = nl // 2
        add("fa_wi", p["ffn_w_in"][nl, 0])
        add("fa_wo", p["ffn_w_out"][nl, 0])
        add("g_a", p["norm_g"][nl, 0])
        add("g_m", p["norm_g"][nl, 1])
        if next_mix == "hg":
            add("wmi", p["hg_w_in"][jn])
            add("lbl", p["hg_lb_logits"])
            consts.append("rmask")
            outs = dict(HG_A_OUT)
        else:
            add("wmi", p["fox_w_in"][jn])
            add("gq_col", np.tile(p["fox_q_norm_g"][jn], 2)[:, None])
            add("gk_col", np.tile(p["fox_k_norm_g"][jn], 2)[:, None])
            add("bf_col", p["fox_b_f"][jn][:, None])
            consts += ["one_col", "blockones"]
            outs = dict(FOX_A_OUT)
    else:
        outs = {}
    outs["xo"] = XS

    def build(C, dram, st):
        P = C.P
        xb = [P.buf() for _ in range(NT // TS)]
        xb0 = [P.buf() for _ in range(NT // TS)]
        cur = dram["xi"]
        curb = xb0
        if prev_mix is not None:
            g_b = load_gain(C, st, dram["g_b"], "g_b")
            P.fence()
            mix_c_stage(C, cur, dram["xo"], curb, xb, dram["oT"], dram["gTi"], dram["wmo"], P.buf())
            cur, curb = dram["xo"], xb
            ffn_stage(C, cur, dram["xo"], curb, xb, dram["fb_wi"], dram["fb_wo"], g_b)
        ob = P.buf()
        if next_mix is not None:
            g_a = load_gain(C, st, dram["g_a"], "g_a")
            g_m = load_gain(C, st, dram["g_m"], "g_m")
            P.fence()
            ffn_stage(C, cur, dram["xo"], curb, xb, dram["fa_wi"], dram["fa_wo"], g_a)
            cur, curb = dram["xo"], xb
            if next_mix == "hg":
                lb_sb, oml_sb = hgrn_small(C, dram, st, jn)
                hgrn_a_stage(C, cur, curb, dram["wmi"], g_m, lb_sb, oml_sb, dram["qx"], dram["kx"], dram["vx"],
                             dram["sc"], dram["gT"], ob)
            else:
                gq2, gk2, nbf = fox_small(C, dram, st)
                fox_a_stage(C, cur, curb, dram["wmi"], g_m, gq2, gk2, nbf, dram["qx"], dram["kx"], dram["vx"],
                            dram["gT"], dram["lfx"], ob)
        return xb + [ob]

    res = run_launch(build, ins, outs, maps, tuple(consts))
    return res.results


def hg_b_launch(p, A, j):
    maps = []
    bf = ml_dtypes.bfloat16
    for c2 in range(NCORES):
        m = {"qx2": np.zeros((2, 128, SEQ), bf), "kx2": np.zeros((2, 128, SEQ), bf),
             "vx2": np.zeros((2, SEQ, 128), bf), "sc2": np.zeros((2, 128, 3, SEQ // 64), np.float32),
             "ong": np.ascontiguousarray(p["hg_out_norm_g"][j])}
        for r in range(2):
            pi = c2 * 2 + r
            b, h = pi // 8, pi % 8
            for qt in range(4):
                src = A[b * 4 + qt]
                m["qx2"][r][:, qt * NT:(qt + 1) * NT] = src["qx"][h]
                m["kx2"][r][:, qt * NT:(qt + 1) * NT] = src["kx"][h]
                m["vx2"][r][qt * NT:(qt + 1) * NT] = src["vx"][:, h * 128:(h + 1) * 128]
                m["sc2"][r][:, :, qt * 64:(qt + 1) * 64] = src["sc"][h]
        maps.append(m)

    def build(C, dram, st):
        nc, P = C.nc, C.P
        on_sb = sb(st, nc, "on_sb", [128, 1], F32)
        P.dma(on_sb[:], dram["ong"].rearrange("(p o) -> p o", o=1), w=[P.buf()])
        P.fence()
        ib, ob = P.buf(), P.buf()
        hgrn_b_stage(C, dram["qx2"], dram["kx2"], dram["vx2"], dram["sc2"], on_sb, dram["oT2"], ib, ob)
        return [ob]

    ins = {"qx2": ((2, 128, SEQ), BF16), "kx2": ((2, 128, SEQ), BF16), "vx2": ((2, SEQ, 128), BF16),
           "sc2": ((2, 128, 3, SEQ // 64), F32), "ong": ((128,), F32)}
    res = run_launch(build, ins, {"oT2": ((2, 128, SEQ), F32)}, maps, ("ones_f32", "eps_col", "ident_bf", "hmask"))
    B = res.results
    oT = []
    for c in range(NCORES):
        b, qt = c // 4, c % 4
        o = np.zeros((D, NT), np.float32)
        for h in range(8):
            pi = b * 8 + h
            o[h * 128:(h + 1) * 128] = B[pi // 2]["oT2"][pi % 2][:, qt * NT:(qt + 1) * NT]
        oT.append(o)
    return oT


def fox_b_launch(p, A, j):
    maps = []
    bf = ml_dtypes.bfloat16
    gq_rep = np.ascontiguousarray(np.tile(p["fox_q_norm_g"][j][None, :], (128, 1)))
    gk_rep = np.ascontiguousarray(np.tile(p["fox_k_norm_g"][j][None, :], (128, 1)))
    for c2 in range(NCORES):
        m = {"qx4": np.zeros((4, 64, SEQ), bf), "kx4": np.zeros((4, 64, SEQ), bf),
             "vx4": np.zeros((4, SEQ, 64), bf), "lf4": np.zeros((4, SEQ), np.float32),
             "gq_rep": gq_rep, "gk_rep": gk_rep}
        for r in range(4):
            pi = c2 * 4 + r
            b, h = pi // 16, pi % 16
            for qt in range(4):
                src = A[b * 4 + qt]
                m["qx4"][r][:, qt * NT:(qt + 1) * NT] = src["qx"][h * 64:(h + 1) * 64]
                m["kx4"][r][:, qt * NT:(qt + 1) * NT] = src["kx"][h * 64:(h + 1) * 64]
                m["vx4"][r][qt * NT:(qt + 1) * NT] = src["vx"][:, h * 64:(h + 1) * 64]
                m["lf4"][r][qt * NT:(qt + 1) * NT] = src["lfx"][h]
        maps.append(m)

    def build(C, dram, st):
        nc, P = C.nc, C.P
        negm = fox_negm(C, dram, st)
        cq = nc.dram_tensor("cq_scr", [4, 3, SEQ], BF16, kind="Internal").ap()
        ib, ob = P.buf(), P.buf()
        fox_b_stage(C, dram["qx4"], dram["kx4"], dram["vx4"], dram["lf4"], negm, cq, dram["oT4"], ib, ob)
        return [ob]

    ins = {"qx4": ((4, 64, SEQ), BF16), "kx4": ((4, 64, SEQ), BF16), "vx4": ((4, SEQ, 64), BF16),
           "lf4": ((4, SEQ), F32), "gq_rep": ((128, 64), F32), "gk_rep": ((128, 64), F32)}
    res = run_launch(build, ins, {"oT4": ((4, 64, SEQ), F32)}, maps, ("ones_f32", "ident_f32", "sutri", "dmask"))
    B = res.results
    oT = []
    for c in range(NCORES):
        b, qt = c // 4, c % 4
        o = np.zeros((D, NT), np.float32)
        for h in range(16):
            pi = b * 16 + h
            o[h * 64:(h + 1) * 64] = B[pi // 4]["oT4"][pi % 4][:, qt * NT:(qt + 1) * NT]
        oT.append(o)
    return oT


def kernel(x, norm_g, ffn_w_in, ffn_w_out, hg_w_in, hg_lb_logits, hg_out_norm_g, hg_w_out,
           fox_w_in, fox_b_f, fox_q_norm_g, fox_k_norm_g, fox_w_out):
    p = dict(norm_g=np.asarray(norm_g, np.float32), ffn_w_in=np.asarray(ffn_w_in, np.float32),
             ffn_w_out=np.asarray(ffn_w_out, np.float32), hg_w_in=np.asarray(hg_w_in, np.float32),
             hg_lb_logits=np.asarray(hg_lb_logits, np.float32), hg_out_norm_g=np.asarray(hg_out_norm_g, np.float32),
             hg_w_out=np.asarray(hg_w_out, np.float32), fox_w_in=np.asarray(fox_w_in, np.float32),
             fox_b_f=np.asarray(fox_b_f, np.float32), fox_q_norm_g=np.asarray(fox_q_norm_g, np.float32),
             fox_k_norm_g=np.asarray(fox_k_norm_g, np.float32), fox_w_out=np.asarray(fox_w_out, np.float32))
    x = np.asarray(x, np.float32)
    xT = [np.ascontiguousarray(x[c // 4, (c % 4) * NT:(c % 4 + 1) * NT, :].T) for c in range(NCORES)]
    mixers = ["hg", "fox", "hg", "fox"]
    A = token_launch(p, xT, 0, None, mixers[0])
    for layer in range(4):
        xT = [A[c]["xo"] for c in range(NCORES)]
        gT = [A[c]["gT"] for c in range(NCORES)]
        if mixers[layer] == "hg":
            oT = hg_b_launch(p, A, layer // 2)
        else:
            oT = fox_b_launch(p, A, layer // 2)
        nxt = mixers[layer + 1] if layer + 1 < 4 else None
        A = token_launch(p, xT, layer, mixers[layer], nxt, oT=oT, gT=gT)
    out = np.zeros((2, SEQ, D), np.float32)
    for c in range(NCORES):
        out[c // 4, (c % 4) * NT:(c % 4 + 1) * NT, :] = A[c]["xo"].T
    return out
```
